# Optimizing a Trainium2 kernel written in Bass

```python
import jax, jax.numpy as jnp
from jax import lax
import numpy as np

D_MODEL = 1024
BATCH = 8
SEQ = 4096
DEPTH = 4

HEAD_DIM = 64
N_NSA_HEADS = 8
N_NSA_KV = 2
NSA_GROUP = N_NSA_HEADS // N_NSA_KV
N_FOX_HEADS = 8
MIX_WIDTH = (N_NSA_HEADS + N_FOX_HEADS) * HEAD_DIM
N_BRANCH = 3
CMP_LEN = 32
CMP_STRIDE = 16
CMP_HIDDEN = 256
SEL_BLOCK = 64
SEL_TOPK = 16
WINDOW = 512
Q_BLOCK = 128
SEL_Q_BLOCK = 32
ROPE_THETA = 500000.0
ROPE_DIMS = HEAD_DIM // 4
D_FF = 4 * D_MODEL
NORM_EPS = 1e-6
NEG_INF = -1e30
FORCED_SCORE = 1e4
NSA_Q_COLS = N_NSA_HEADS * HEAD_DIM
KV_COLS = N_NSA_KV * HEAD_DIM
NSA_GATE_COLS = N_NSA_HEADS * N_BRANCH
FOX_COLS = N_FOX_HEADS * HEAD_DIM
FOX_F_COLS = N_FOX_HEADS
COL_SIZES = (NSA_Q_COLS, KV_COLS, KV_COLS, KV_COLS, KV_COLS, KV_COLS, KV_COLS, NSA_GATE_COLS, FOX_COLS, FOX_COLS, FOX_COLS, FOX_F_COLS)
IN_COLS = NSA_Q_COLS + 6 * KV_COLS + NSA_GATE_COLS + 3 * FOX_COLS + FOX_F_COLS

kernel_name = "nsa_fox_hymba_sandwich_trunk"


def _rms_norm(x, g):
    xf = x.astype(jnp.float32)
    y = xf * lax.rsqrt(jnp.mean(xf * xf, axis=-1, keepdims=True) + NORM_EPS)
    return (y * g.astype(jnp.float32)).astype(x.dtype)


def _partial_rope(x, pos):
    half = ROPE_DIMS // 2
    inv_freq = ROPE_THETA ** (-jnp.arange(half, dtype=jnp.float32) * 2.0 / ROPE_DIMS)
    ang = pos.astype(jnp.float32)[:, None] * inv_freq[None, :]
    cos, sin = jnp.cos(ang), jnp.sin(ang)
    xf = x.astype(jnp.float32)
    x1, x2 = xf[..., :half], xf[..., half:ROPE_DIMS]
    out = jnp.concatenate([x1 * cos - x2 * sin, x1 * sin + x2 * cos, xf[..., ROPE_DIMS:]], axis=-1)
    return out.astype(x.dtype)


def _masked_softmax(s, mask):
    return jax.nn.softmax(jnp.where(mask, s, NEG_INF), axis=-1)


def _unblock(t, axis):
    t = jnp.moveaxis(t, 0, axis)
    shp = t.shape
    return t.reshape(shp[:axis] + (shp[axis] * shp[axis + 1],) + shp[axis + 2:])


def _importance_map(seq):
    n_cmp = (seq - CMP_LEN) // CMP_STRIDE + 1
    n_sel = seq // SEL_BLOCK
    cs = np.arange(n_cmp)[:, None] * CMP_STRIDE
    ss = np.arange(n_sel)[None, :] * SEL_BLOCK
    overlap = np.clip(np.minimum(cs + CMP_LEN, ss + SEL_BLOCK) - np.maximum(cs, ss), 0, None)
    return jnp.asarray(overlap / CMP_LEN, dtype=jnp.float32)


def _compress(kv, pos_emb, w1, b1, w2, b2):
    b, g, s, d = kv.shape
    r = CMP_LEN // CMP_STRIDE
    n_chunk = s // CMP_STRIDE
    n_cmp = n_chunk - r + 1
    chunks = kv.reshape(b, g, n_chunk, CMP_STRIDE, d)
    blocks = jnp.concatenate([chunks[:, :, j:j + n_cmp] for j in range(r)], axis=3)
    blocks = (blocks + pos_emb).reshape(b, g, n_cmp, CMP_LEN * d)
    hid = jax.nn.gelu(blocks @ w1 + b1)
    return hid @ w2 + b2


def _nsa_mixer(q, k_c, v_c, k_s, v_s, k_w, v_w, gate_logits, gate_bias,
               pos_k, w1_k, b1_k, w2_k, b2_k, pos_v, w1_v, b1_v, w2_v, b2_v):
    b, s, _, d = q.shape
    pos = jnp.arange(s)
    scale = HEAD_DIM ** -0.5
    q = q.reshape(b, s, N_NSA_KV, NSA_GROUP, d).transpose(0, 2, 3, 1, 4)
    q_rot = _partial_rope(q, pos)
    k_c, v_c, k_s, v_s, k_w, v_w = (t.transpose(0, 2, 1, 3) for t in (k_c, v_c, k_s, v_s, k_w, v_w))
    kc = _compress(k_c, pos_k, w1_k, b1_k, w2_k, b2_k)
    vc = _compress(v_c, pos_v, w1_v, b1_v, w2_v, b2_v)
    n_cmp = kc.shape[2]
    cmp_end = jnp.arange(n_cmp) * CMP_STRIDE + CMP_LEN - 1
    imp_map = _importance_map(s)
    n_sel = s // SEL_BLOCK
    top_k = min(SEL_TOPK, n_sel)
    k_s_rot = _partial_rope(k_s, pos)
    pad = ((0, 0), (0, 0), (WINDOW, 0), (0, 0))
    k_w_pad = jnp.pad(_partial_rope(k_w, pos), pad)
    v_w_pad = jnp.pad(v_w, pad)
    blk = jnp.arange(n_sel)

    def block_fn(i):
        q0 = i * Q_BLOCK
        qpos = q0 + jnp.arange(Q_BLOCK)
        qp = lax.dynamic_slice_in_dim(q, q0, Q_BLOCK, axis=3)
        qr = lax.dynamic_slice_in_dim(q_rot, q0, Q_BLOCK, axis=3)
        sc = jnp.einsum('bghqd,bgnd->bghqn', qp, kc).astype(jnp.float32) * scale
        mc = cmp_end[None, :] <= qpos[:, None]
        pc = _masked_softmax(sc, mc) * mc
        o_cmp = jnp.einsum('bghqn,bgnd->bghqd', pc.astype(vc.dtype), vc)
        imp = jnp.einsum('bghqn,nj->bgqj', pc, imp_map)
        cur = qpos // SEL_BLOCK
        valid = blk[None, :] * SEL_BLOCK <= qpos[:, None]
        forced = (blk[None, :] == 0) | (blk[None, :] == cur[:, None]) | (blk[None, :] == cur[:, None] - 1)
        score = jnp.where(forced, FORCED_SCORE, jnp.where(valid, imp, -1.0))
        _, idx = lax.top_k(score, top_k)
        kw = lax.dynamic_slice_in_dim(k_w_pad, q0, WINDOW + Q_BLOCK, axis=2)
        vw = lax.dynamic_slice_in_dim(v_w_pad, q0, WINDOW + Q_BLOCK, axis=2)
        kpos = q0 - WINDOW + jnp.arange(WINDOW + Q_BLOCK)
        mw = (kpos[None, :] <= qpos[:, None]) & (kpos[None, :] > qpos[:, None] - WINDOW) & (kpos[None, :] >= 0)
        sw = jnp.einsum('bghqd,bgkd->bghqk', qr, kw).astype(jnp.float32) * scale
        pw = _masked_softmax(sw, mw)
        o_win = jnp.einsum('bghqk,bgkd->bghqd', pw.astype(vw.dtype), vw)
        return o_cmp, o_win, idx

    o_cmp, o_win, sel_idx = lax.map(block_fn, jnp.arange(s // Q_BLOCK))
    o_cmp = _unblock(o_cmp, 3)
    o_win = _unblock(o_win, 3)
    sel_idx = _unblock(sel_idx, 2)

    kb = k_s_rot.reshape(b, N_NSA_KV, n_sel, SEL_BLOCK, d)
    vb = v_s.reshape(b, N_NSA_KV, n_sel, SEL_BLOCK, d)
    gather = jax.vmap(jax.vmap(lambda blocks, ix: blocks[ix]))
    n_keys = top_k * SEL_BLOCK

    def sel_fn(c):
        q0 = c * SEL_Q_BLOCK
        qpos = q0 + jnp.arange(SEL_Q_BLOCK)
        qr = lax.dynamic_slice_in_dim(q_rot, q0, SEL_Q_BLOCK, axis=3)
        ix = lax.dynamic_slice_in_dim(sel_idx, q0, SEL_Q_BLOCK, axis=2)
        kg = gather(kb, ix)
        vg = gather(vb, ix).reshape(b, N_NSA_KV, SEL_Q_BLOCK, n_keys, d)
        kpos = ix[..., None] * SEL_BLOCK + jnp.arange(SEL_BLOCK)
        m = (kpos <= qpos[None, None, :, None, None]).reshape(b, N_NSA_KV, 1, SEL_Q_BLOCK, n_keys)
        ss = jnp.einsum('bghqd,bgqnkd->bghqnk', qr, kg).astype(jnp.float32) * scale
        ps = _masked_softmax(ss.reshape(b, N_NSA_KV, NSA_GROUP, SEL_Q_BLOCK, n_keys), m)
        return jnp.einsum('bghqm,bgqmd->bghqd', ps.astype(vg.dtype), vg)

    o_sel = _unblock(lax.map(sel_fn, jnp.arange(s // SEL_Q_BLOCK)), 3)

    g = jax.nn.sigmoid((gate_logits + gate_bias).astype(jnp.float32)).astype(q.dtype)
    g = g.reshape(b, s, N_NSA_KV, NSA_GROUP, N_BRANCH).transpose(0, 2, 3, 1, 4)
    o = g[..., 0:1] * o_cmp + g[..., 1:2] * o_sel + g[..., 2:3] * o_win
    return o.transpose(0, 3, 1, 2, 4).reshape(b, s, N_NSA_HEADS * d)


def _fox_mixer(q, k, v, f_logits, f_bias):
    b, s, h, d = q.shape
    scale = HEAD_DIM ** -0.5
    q, k, v = (t.transpose(0, 2, 1, 3) for t in (q, k, v))
    log_f = jax.nn.log_sigmoid(f_logits.astype(jnp.float32) + f_bias.astype(jnp.float32))
    cum = jnp.cumsum(log_f, axis=1).transpose(0, 2, 1)
    outs = []
    for i in range(s // Q_BLOCK):
        q0, q1 = i * Q_BLOCK, (i + 1) * Q_BLOCK
        logits = (jnp.einsum('bhqd,bhkd->bhqk', q[:, :, q0:q1], k[:, :, :q1]).astype(jnp.float32) * scale
                  + cum[:, :, q0:q1, None] - cum[:, :, None, :q1])
        mask = jnp.arange(q1)[None, :] <= jnp.arange(q0, q1)[:, None]
        p = _masked_softmax(logits, mask)
        outs.append(jnp.einsum('bhqk,bhkd->bhqd', p.astype(v.dtype), v[:, :, :q1]))
    o = jnp.concatenate(outs, axis=2)
    return o.transpose(0, 2, 1, 3).reshape(b, s, h * d)


def setup_inputs(seed: int = 0) -> dict:
    key = jax.random.key(seed)
    ks = jax.random.split(key, 22)
    f32 = jnp.float32
    nrm = lambda k, shape, fan_in: jax.random.normal(k, shape, f32) * (fan_in ** -0.5)
    small = lambda k, shape, sc: sc * jax.random.normal(k, shape, f32)
    gain = lambda k: 1.0 + 0.05 * jax.random.normal(k, (DEPTH, D_MODEL), f32)
    cmp_in = CMP_LEN * HEAD_DIM
    return {
        "x": jax.random.normal(ks[0], (BATCH, SEQ, D_MODEL), f32),
        "w_in": nrm(ks[1], (DEPTH, D_MODEL, IN_COLS), D_MODEL),
        "b_nsa_gate": small(ks[2], (DEPTH, NSA_GATE_COLS), 0.01),
        "b_forget": jax.random.uniform(ks[3], (DEPTH, N_FOX_HEADS), f32, minval=1.0, maxval=6.0),
        "cmp_pos_k": small(ks[4], (DEPTH, CMP_LEN, HEAD_DIM), 0.1),
        "cmp_w1_k": nrm(ks[5], (DEPTH, cmp_in, CMP_HIDDEN), cmp_in),
        "cmp_b1_k": small(ks[6], (DEPTH, CMP_HIDDEN), 0.01),
        "cmp_w2_k": nrm(ks[7], (DEPTH, CMP_HIDDEN, HEAD_DIM), CMP_HIDDEN),
        "cmp_b2_k": small(ks[8], (DEPTH, HEAD_DIM), 0.01),
        "cmp_pos_v": small(ks[9], (DEPTH, CMP_LEN, HEAD_DIM), 0.1),
        "cmp_w1_v": nrm(ks[10], (DEPTH, cmp_in, CMP_HIDDEN), cmp_in),
        "cmp_b1_v": small(ks[11], (DEPTH, CMP_HIDDEN), 0.01),
        "cmp_w2_v": nrm(ks[12], (DEPTH, CMP_HIDDEN, HEAD_DIM), CMP_HIDDEN),
        "cmp_b2_v": small(ks[13], (DEPTH, HEAD_DIM), 0.01),
        "w_out": nrm(ks[14], (DEPTH, MIX_WIDTH, D_MODEL), MIX_WIDTH),
        "w_up": nrm(ks[15], (DEPTH, D_MODEL, D_FF), D_MODEL),
        "w_down": nrm(ks[16], (DEPTH, D_FF, D_MODEL), D_FF),
        "g_pre_mix": gain(ks[17]),
        "g_post_mix": gain(ks[18]),
        "g_pre_mlp": gain(ks[19]),
        "g_post_mlp": gain(ks[20]),
    }


def reference(x, w_in, b_nsa_gate, b_forget, cmp_pos_k, cmp_w1_k, cmp_b1_k, cmp_w2_k, cmp_b2_k,
              cmp_pos_v, cmp_w1_v, cmp_b1_v, cmp_w2_v, cmp_b2_v, w_out, w_up, w_down,
              g_pre_mix, g_post_mix, g_pre_mlp, g_post_mlp):
    b, s, _ = x.shape
    split_at = [int(v) for v in np.cumsum(COL_SIZES)[:-1]]
    for l in range(DEPTH):
        h = _rms_norm(x, g_pre_mix[l])
        parts = jnp.split(h @ w_in[l], split_at, axis=-1)
        nq, kc, vc, ksel, vsel, kwin, vwin, gates, fq, fk, fv, ff = parts
        heads = lambda t, n: t.reshape(b, s, n, HEAD_DIM)
        o_nsa = _nsa_mixer(heads(nq, N_NSA_HEADS),
                           heads(kc, N_NSA_KV), heads(vc, N_NSA_KV),
                           heads(ksel, N_NSA_KV), heads(vsel, N_NSA_KV),
                           heads(kwin, N_NSA_KV), heads(vwin, N_NSA_KV),
                           gates, b_nsa_gate[l],
                           cmp_pos_k[l], cmp_w1_k[l], cmp_b1_k[l], cmp_w2_k[l], cmp_b2_k[l],
                           cmp_pos_v[l], cmp_w1_v[l], cmp_b1_v[l], cmp_w2_v[l], cmp_b2_v[l])
        o_fox = _fox_mixer(heads(fq, N_FOX_HEADS), heads(fk, N_FOX_HEADS), heads(fv, N_FOX_HEADS),
                           ff, b_forget[l])
        mix = jnp.concatenate([o_nsa, o_fox], axis=-1) @ w_out[l]
        x = x + _rms_norm(mix, g_post_mix[l])
        h = _rms_norm(x, g_pre_mlp[l])
        y = jnp.square(jax.nn.relu(h @ w_up[l])) @ w_down[l]
        x = x + _rms_norm(y, g_post_mlp[l])
    return x
```

```python
import numpy as np
import ml_dtypes
import concourse.bass as bass
import concourse.mybir as mybir
from concourse.bass_utils import run_bass_kernel_spmd

F32 = mybir.dt.float32
BF16 = mybir.dt.bfloat16
AF = mybir.ActivationFunctionType
ALU = mybir.AluOpType

SEQ = 4096
DM = 1024
NT = 32
DEPTH = 4
INC = 2848
DFF = 4096
EPS = 1e-6
NEGM = -30000.0
NT_LIMIT = None
import os as _os
DBG_SKIP = _os.environ.get('DBG_SKIP', '')

ENGS = ("pe", "act", "dve", "pool")


class Buf:
    __slots__ = ("name", "lw", "rd")

    def __init__(self, name):
        self.name = name
        self.lw = None
        self.rd = []


class Ins:
    __slots__ = ("eng", "fn", "deps", "need_inc", "sem", "val", "isdma", "idx")


class Sched:
    def __init__(self, nc, n_dma_sems=48, strict=True):
        self.nc = nc
        self.eng = {"pe": nc.tensor, "act": nc.scalar, "dve": nc.vector,
                    "pool": nc.gpsimd, "sp": nc.sync}
        self.ins = []
        self.strict = strict
        self.n_dma_sems = n_dma_sems
        self.dma_rr = 0
        self.dma_last = [None] * n_dma_sems
        self.dma_cnt = [0] * n_dma_sems
        self.bar_deps = []
        self.bar_need = set()
        self.bar_mark = 0

    def barrier(self):
        last = {}
        deps = []
        for ins in self.ins[self.bar_mark:]:
            if ins.isdma:
                deps.append(ins)
            else:
                last[ins.eng] = ins
        for d in self.bar_deps:
            if not d.isdma and d.eng not in last:
                last[d.eng] = d
        self.bar_deps = deps + list(last.values())
        self.bar_need = set(self.eng.keys())
        self.bar_mark = len(self.ins)

    def _bar(self, ins):
        if ins.eng in self.bar_need:
            self.bar_need.discard(ins.eng)
            ins.deps.extend(self.bar_deps)

    def _deps(self, reads, writes):
        deps = []
        for b in reads:
            if b.lw is not None:
                deps.append(b.lw)
        for b in writes:
            if b.lw is not None:
                deps.append(b.lw)
            deps.extend(b.rd)
        return deps

    def _commit(self, ins, reads, writes):
        for b in writes:
            b.lw = ins
            b.rd = []
        for b in reads:
            if b.lw is not ins:
                if not ins.isdma:
                    b.rd = [x for x in b.rd if x.isdma or x.eng != ins.eng]
                b.rd.append(ins)

    def op(self, eng, fn, reads=(), writes=()):
        ins = Ins()
        ins.eng = eng
        ins.fn = fn
        ins.isdma = False
        ins.need_inc = False
        ins.sem = None
        ins.val = 0
        ins.deps = self._deps(reads, writes)
        self._bar(ins)
        ins.idx = len(self.ins)
        self.ins.append(ins)
        self._commit(ins, reads, writes)
        return ins

    def dma(self, out, in_, reads=(), writes=(), queue="sp"):
        ins = Ins()
        ins.eng = queue
        nc_eng = self.eng[queue]
        ins.fn = lambda: nc_eng.dma_start(out=out, in_=in_)
        ins.isdma = True
        ins.need_inc = True
        k = self.dma_rr
        self.dma_rr = (k + 1) % self.n_dma_sems
        ins.deps = self._deps(reads, writes)
        if self.dma_last[k] is not None:
            ins.deps.append(self.dma_last[k])
        self.dma_last[k] = ins
        self.dma_cnt[k] += 1
        ins.sem = ("dma", k)
        ins.val = 16 * self.dma_cnt[k]
        self._bar(ins)
        ins.idx = len(self.ins)
        self.ins.append(ins)
        self._commit(ins, reads, writes)
        return ins

    def emit(self, final_wait=()):
        nc = self.nc
        for ins in self.ins:
            for d in ins.deps:
                if d.isdma:
                    continue
                if d.eng != ins.eng:
                    d.need_inc = True
                elif self.strict and d.eng != "pe":
                    d.need_inc = True
        cnt = {e: 0 for e in ENGS}
        for ins in self.ins:
            if ins.isdma:
                continue
            if ins.need_inc:
                cnt[ins.eng] += 1
            ins.sem = ins.eng
            ins.val = cnt[ins.eng]
        sems = {}
        for e in ENGS:
            sems[e] = nc.alloc_semaphore("s_" + e)
        for k in range(self.n_dma_sems):
            sems[("dma", k)] = nc.alloc_semaphore("s_dma%d" % k)
        seen = {e: {} for e in self.eng}
        nwaits = 0
        for ins in self.ins:
            e = ins.eng
            sd = seen[e]
            eng = self.eng[e]
            need = {}
            for d in ins.deps:
                if not d.isdma:
                    if d.eng == e and (not self.strict or e == "pe"):
                        continue
                if sd.get(d.sem, 0) >= d.val:
                    continue
                if need.get(d.sem, 0) < d.val:
                    need[d.sem] = d.val
            for s, v in need.items():
                eng.wait_ge(sems[s], v)
                sd[s] = v
                nwaits += 1
            r = ins.fn()
            if ins.isdma:
                r.then_inc(sems[ins.sem], 16)
            elif ins.need_inc:
                r.then_inc(sems[ins.sem], 1)
        eng = self.eng["sp"]
        need = {}
        for d in final_wait:
            if need.get(d.sem, 0) < d.val:
                need[d.sem] = d.val
        for s, v in need.items():
            eng.wait_ge(sems[s], v)
        self.stats = dict(n_ins=len(self.ins), n_waits=nwaits, cnt=cnt)


class Rot:
    def __init__(self, items):
        self.items = items
        self.i = 0

    def next(self):
        it = self.items[self.i]
        self.i = (self.i + 1) % len(self.items)
        return it


class K:
    def __init__(self, n_layers=DEPTH, dbg=None, strict=True, phases="ABCDEFG"):
        self.n_layers = n_layers
        self.dbg = dbg or ()
        self.phases = phases
        self.nc = nc = bass.Bass("TRN2", target_bir_lowering=False)
        self.S = Sched(nc, strict=strict)
        self.dbufs = {}
        self.sb_n = 0

    def db(self, name, idx=0):
        k = (name, idx)
        b = self.dbufs.get(k)
        if b is None:
            b = self.dbufs[k] = Buf("%s_%s" % (name, idx))
        return b

    def dbs(self, name, lo, hi):
        return [self.db(name, i) for i in range(lo, hi)]

    def sb(self, shape, dt, name=None):
        self.sb_n += 1
        nm = "%s_%d" % (name or "t", self.sb_n)
        t = self.nc.alloc_sbuf_tensor(nm, list(shape), dt)
        return t, Buf(nm)

    def rot(self, n, shape, dt, name=None):
        return Rot([self.sb(shape, dt, name) for _ in range(n)])

    def mm(self, out, lhsT, rhs, start=True, stop=True, r=(), w=()):
        nc = self.nc
        return self.S.op("pe", lambda: nc.tensor.matmul(out, lhsT=lhsT, rhs=rhs, start=start, stop=stop,
                                                         skip_group_check=True), r, w)

    def tr(self, out, in_, ident, r=(), w=()):
        nc = self.nc
        return self.S.op("pe", lambda: nc.tensor.transpose(out, in_, ident), r, w)

    def act(self, out, in_, func, r=(), w=(), bias=None, scale=None, accum_out=None):
        nc = self.nc
        kw = {}
        if bias is not None:
            kw["bias"] = bias
        if scale is not None:
            kw["scale"] = scale
        if accum_out is not None:
            kw["accum_out"] = accum_out
        return self.S.op("act", lambda: nc.scalar.activation(out=out, in_=in_, func=func, **kw), r, w)

    def _ve(self, eng):
        return self.nc.vector if eng == "dve" else self.nc.gpsimd

    def ts(self, eng, out, in0, s1, s2, op0, op1=None, r=(), w=()):
        e = self._ve(eng)
        if op1 is None:
            return self.S.op(eng, lambda: e.tensor_scalar(out=out, in0=in0, scalar1=s1, scalar2=None, op0=op0), r, w)
        return self.S.op(eng, lambda: e.tensor_scalar(out=out, in0=in0, scalar1=s1, scalar2=s2, op0=op0, op1=op1), r, w)

    def tt(self, eng, out, in0, in1, op, r=(), w=()):
        e = self._ve(eng)
        return self.S.op(eng, lambda: e.tensor_tensor(out=out, in0=in0, in1=in1, op=op), r, w)

    def stt(self, out, in0, scalar, in1, op0, op1, r=(), w=()):
        self.ts("dve", out, in0, scalar, None, op0, r=r, w=w)
        return self.tt("dve", out, out, in1, op1, r=list(r) + list(w), w=w)

    def cp(self, eng, out, in_, r=(), w=()):
        nc = self.nc
        if eng == "act":
            return self.S.op("act", lambda: nc.scalar.copy(out=out, in_=in_), r, w)
        e = self._ve(eng)
        return self.S.op(eng, lambda: e.tensor_copy(out=out, in_=in_), r, w)

    def recip(self, out, in_, r=(), w=()):
        nc = self.nc
        return self.S.op("dve", lambda: nc.vector.reciprocal(out=out, in_=in_), r, w)

    def memset(self, eng, ap, val, w=()):
        e = self._ve(eng)
        return self.S.op(eng, lambda: e.memset(ap, val), (), w)

    def aff(self, out, in_, pattern, base, cm, r=(), w=()):
        nc = self.nc
        return self.S.op("pool", lambda: nc.gpsimd.affine_select(out=out, in_=in_, pattern=pattern, compare_op=ALU.is_ge,
                                                                 fill=0.0, base=base, channel_multiplier=cm), r, w)

    def dma(self, out, in_, r=(), w=()):
        return self.S.dma(out, in_, r, w)

    def build(self):
        nc = self.nc
        dt = nc.dram_tensor
        I = {}
        DEPTH = self.n_layers
        specs = [("x", [SEQ, DM]), ("w_in", [DEPTH, DM, INC]), ("b_nsa_gate", [DEPTH, 24]), ("b_forget", [DEPTH, 8]),
                 ("cmp_pos_k", [DEPTH, 32, 64]), ("cmp_w1_k", [DEPTH, 2048, 256]), ("cmp_b1_k", [DEPTH, 256]),
                 ("cmp_w2_k", [DEPTH, 256, 64]), ("cmp_b2_k", [DEPTH, 64]),
                 ("cmp_pos_v", [DEPTH, 32, 64]), ("cmp_w1_v", [DEPTH, 2048, 256]), ("cmp_b1_v", [DEPTH, 256]),
                 ("cmp_w2_v", [DEPTH, 256, 64]), ("cmp_b2_v", [DEPTH, 64]),
                 ("w_out", [DEPTH, DM, DM]), ("w_up", [DEPTH, DM, DFF]), ("w_down", [DEPTH, DFF, DM]),
                 ("g_pre_mix", [DEPTH, DM]), ("g_post_mix", [DEPTH, DM]), ("g_pre_mlp", [DEPTH, DM]),
                 ("g_post_mlp", [DEPTH, DM]),
                 ("c_cos", [128, NT * 8]), ("c_sin", [128, NT * 8]), ("c_valid", [128, NT * 64]), ("c_add", [128, NT * 64])]
        for n, s in specs:
            I[n] = dt(n, s, F32, kind="ExternalInput").ap()
        for n, s in [("c_imp", [256, 65]), ("c_blk", [64, SEQ]), ("c_idb", [128, 128]), ("c_oh", [128, 24 * 128]), ("c_tri", [128, 512]), ("c_tric", [128, 512]), ("c_cmpm", [128, 17 * 512])]:
            I[n] = dt(n, s, BF16, kind="ExternalInput").ap()
        I["c_idf"] = dt("c_idf", [128, 128], F32, kind="ExternalInput").ap()
        self.I = I
        self.out = dt("out", [SEQ, DM], F32, kind="ExternalOutput").ap()
        D = {}
        for n, s, d in [("XMID", [SEQ, DM], F32), ("XA", [SEQ, DM], F32), ("XB", [SEQ, DM], F32),
                        ("QUT", [512, SEQ], BF16), ("QRT", [512, SEQ], BF16), ("KVT", [512, SEQ], BF16),
                        ("FQT", [512, SEQ], BF16), ("FKT", [512, SEQ], BF16),
                        ("FV", [SEQ, 512], BF16), ("VS", [SEQ, 128], BF16), ("VW", [SEQ, 128], BF16),
                        ("FKA", [8, 4, SEQ], BF16), ("FQA", [8, 4, SEQ], BF16), ("MIXT", [DM, SEQ], BF16)]:
            if n in self.dbg:
                D[n] = dt(n, s, d, kind="ExternalOutput").ap()
            else:
                D[n] = dt(n, s, d).ap()
        self.D = D
        self.dbg_out = {}

        self.PA = nc.alloc_psum_tensor("PA", [128, 1024], F32)
        self.PB = nc.alloc_psum_tensor("PB", [128, 1024], F32)
        self.PQ = [nc.alloc_psum_tensor("PQ%d" % i, [128, 512], F32) for i in range(4)]
        self.bPA = Buf("PA")
        self.bPB = Buf("PB")
        self.bPQ = [Buf("PQ%d" % i) for i in range(4)]
        self.bPAh = [Buf("PA0"), Buf("PA1")]
        self.bPBh = [Buf("PB0"), Buf("PB1")]

        self.idb, self.b_idb = self.sb([128, 128], BF16, "idb")
        self.idf, self.b_idf = self.sb([128, 128], F32, "idf")
        self.dma(self.idb[:], I["c_idb"][:, :], w=[self.b_idb])
        self.dma(self.idf[:], I["c_idf"][:, :], w=[self.b_idf])

        fin = []
        for l in range(self.n_layers):
            xin = I["x"] if l == 0 else (D["XA"] if l % 2 == 1 else D["XB"])
            xin_n = "x" if l == 0 else ("XA" if l % 2 == 1 else "XB")
            last = (l == self.n_layers - 1)
            xout = self.out if last else (D["XA"] if l % 2 == 0 else D["XB"])
            xout_n = "out" if last else ("XA" if l % 2 == 0 else "XB")
            top = nc.sbuf_top() if callable(getattr(nc, "sbuf_top", None)) else None
            self.layer(l, xin, xin_n, xout, xout_n)
        for n in self.dbg_out:
            pass
        fin = [i for i in self.S.ins if i.isdma]
        with nc.allow_non_contiguous_dma(reason="small strided parameter loads"):
            self.S.emit(final_wait=fin[-64:])
        return nc

    def layer(self, l, xin, xin_n, xout, xout_n):
        nc = self.nc
        I, D = self.I, self.D
        import contextlib
        with contextlib.ExitStack() as ls:
            def lsb(shape, dtp, name):
                self.sb_n += 1
                nm = "%s_%d" % (name, self.sb_n)
                t = ls.enter_context(nc.sbuf_tensor(nm, list(shape), dtp))
                return t, Buf(nm)
            GT, bGT = lsb([128, SEQ], BF16, "GT")
            LFT, bLFT = lsb([8, SEQ], F32, "LFT")
            LF2, bLF2 = lsb([8, SEQ], F32, "LF2")
            if "A" in self.phases:
                self.phase_a(l, xin, xin_n, GT, bGT, LFT, bLFT)
                self.S.barrier()
            if "B" in self.phases:
                self.ts("dve", LFT[:], GT[32:40, :], -1.0, None, ALU.mult, r=[bGT], w=[bLFT])
                self.cp("dve", LF2[:], GT[64:72, :], r=[bGT], w=[bLF2])
                self.tt("dve", LFT[:], LFT[:], LF2[:], ALU.subtract, r=[bLF2, bLFT], w=[bLFT])
                self.cp("dve", LF2[:], GT[96:104, :], r=[bGT], w=[bLF2])
                self.tt("dve", LFT[:], LFT[:], LF2[:], ALU.subtract, r=[bLF2, bLFT], w=[bLFT])
                self.phase_b(l, LFT, bLFT)
                self.S.barrier()
            if "C" in self.phases:
                self.phase_nsa(l, GT, bGT)
                self.S.barrier()
        if "E" in self.phases:
            self.phase_fox(l)
            self.S.barrier()
        if "F" in self.phases:
            self.phase_f(l, xin, xin_n)
            self.S.barrier()
        if "G" in self.phases:
            self.phase_g(l, xout, xout_n)
            self.S.barrier()

    def rstd_from_ss(self, ss, bss, rs, brs):
        self.ts("dve", ss, ss, 1.0 / DM, EPS, ALU.mult, ALU.add, r=[bss], w=[bss])
        self.act(rs, ss, AF.Ln, r=[bss], w=[brs])
        self.act(rs, rs, AF.Exp, r=[brs], w=[brs], scale=-0.5)

    def phase_a(self, l, xin, xin_n, GT, bGT, LFT, bLFT):
        nc = self.nc
        I, D = self.I, self.D
        import contextlib
        with contextlib.ExitStack() as es:
            def sb(shape, dtp, name):
                self.sb_n += 1
                nm = "%s_%d" % (name, self.sb_n)
                t = es.enter_context(nc.sbuf_tensor(nm, list(shape), dtp))
                return t, Buf(nm)

            def rot(n, shape, dtp, name):
                return Rot([sb(shape, dtp, name) for _ in range(n)])
            W, bW = sb([128, 8, INC], BF16, "Win")
            stg = rot(2, [128, INC], F32, "wstg")
            gpre, bgpre = sb([128, 8], F32, "gpre")
            cosT, bcos = sb([128, NT, 8], F32, "cos")
            sinT, bsin = sb([128, NT, 8], F32, "sin")
            bgate, bbgate = sb([128, 24], F32, "bgate")
            bfor, bbfor = sb([128, 8], F32, "bfor")
            self.dma(gpre[:], I["g_pre_mix"][l].rearrange("(c p) -> p c", p=128), w=[bgpre])
            self.dma(cosT[:].rearrange("p t j -> p (t j)"), I["c_cos"][:, :], w=[bcos])
            self.dma(sinT[:].rearrange("p t j -> p (t j)"), I["c_sin"][:, :], w=[bsin])
            self.dma(bgate[:], I["b_nsa_gate"][l:l + 1, :].partition_broadcast(128), w=[bbgate])
            self.dma(bfor[:], I["b_forget"][l:l + 1, :].partition_broadcast(128), w=[bbfor])
            segs = [(0, 896, 0), (1024, 1152, 896), (1304, 2840, 1024), (896, 1024, 2560), (1152, 1304, 2688),
                    (2840, 2848, 2840)]
            for c in range(8):
                st, bst = stg.next()
                for (a, b, n0) in segs:
                    self.dma(st[:, n0:n0 + (b - a)], I["w_in"][l, c * 128:(c + 1) * 128, a:b], w=[bst])
                self.ts("dve", W[:, c, :], st[:], gpre[:, c:c + 1], None, ALU.mult, r=[bst, bgpre], w=[bW])

            xts = rot(2, [128, DM], F32, "xt")
            xbs = rot(2, [128, DM], BF16, "xb")
            xTs = rot(2, [128, 8, 128], BF16, "xT")
            junk, bjunk = sb([128, DM], BF16, "junk")
            sss = rot(2, [128, 2], F32, "ss")
            Qf, bQf = sb([128, 512], F32, "Qf")
            Kf, bKf = sb([128, 512], F32, "Kf")
            Qus = rot(2, [128, 512], BF16, "Qu")
            Qrs = rot(2, [128, 512], BF16, "Qr")
            K1s = rot(2, [128, 512], BF16, "K1")
            Fqs = rot(2, [128, 512], BF16, "Fq")
            Fks = rot(2, [128, 512], BF16, "Fk")
            Fvs = rot(2, [128, 512], BF16, "Fv")
            V2s = rot(2, [128, 256], BF16, "V2")
            tmp, btmp = sb([128, 8, 32], F32, "ropetmp")
            gz, bgz = sb([128, 40], F32, "gz")
            self.memset("dve", gz[:], 0.0, w=[bgz])
            gzb, bgzb = sb([128, 128], BF16, "gzb")
            self.memset("dve", gzb[:], 0.0, w=[bgzb])
            r1t, br1t = sb([128, 8], F32, "r1t")
            Tst = rot(4, [128, 4, 128], BF16, "Tst")

            PA, PB, PQ = self.PA, self.PB, self.PQ
            psT = PA[:, 0:512].bitcast(BF16)
            bpsT = self.bPAh[0]
            psT2 = [PA[:, 512:1024].bitcast(BF16), PB[:, 0:512].bitcast(BF16)]
            bpsT2 = [self.bPAh[1], self.bPBh[0]]
            psF = PB[:, 512:1024]
            bpsF = self.bPBh[1]
            t2i = 0
            for t in range(NT if NT_LIMIT is None else NT_LIMIT):
                xt, bxt = xts.next()
                xb, bxb = xbs.next()
                xT, bxT = xTs.next()
                ss, bss = sss.next()
                tok = slice(t * 128, (t + 1) * 128)
                self.dma(xt[:], xin[tok, :], r=[self.db(xin_n, t)], w=[bxt])
                self.act(junk[:], xt[:], AF.Square, r=[bxt], w=[bjunk, bss], accum_out=ss[:, 0:1])
                self.rstd_from_ss(ss[:, 0:1], bss, ss[:, 1:2], bss)
                rstd = ss[:, 1:2]
                self.cp("dve", xb[:], xt[:], r=[bxt], w=[bxb])
                for c in range(8):
                    self.tr(psT[:, c * 128:(c + 1) * 128], xb[:, c * 128:(c + 1) * 128], self.idb[:],
                            r=[bxb, self.b_idb], w=[bpsT])
                self.cp("dve", xT[:].rearrange("p c q -> p (c q)"), psT[:, :], r=[bpsT], w=[bxT])
                if 'C' in DBG_SKIP:
                    self.dma(D["FV"][tok, :], xb[:, 0:512], r=[bxb], w=[self.db("FV", t)])
                    continue
                cgs = [(0, 512), (512, 1024), (1024, 1536), (1536, 2048), (2048, 2560), (2560, 2848)]
                Qu, bQu = Qus.next()
                Qr, bQr = Qrs.next()
                K1, bK1 = K1s.next()
                Fq, bFq = Fqs.next()
                Fk, bFk = Fks.next()
                Fv, bFv = Fvs.next()
                V2, bV2 = V2s.next()
                for gi, (c0, c1) in enumerate(cgs):
                    ps = PQ[gi % 4]
                    bps = self.bPQ[gi % 4]
                    n = c1 - c0
                    for c in range(8):
                        self.mm(ps[:, 0:n], xT[:, c, :], W[:, c, c0:c1], start=(c == 0), stop=(c == 7),
                                r=[bxT, bW], w=[bps])
                    if gi == 0:
                        self.act(Qf[:], ps[:, 0:512], AF.Copy, r=[bps, bss], w=[bQf], scale=rstd)
                        self.cp("act", Qu[:], Qf[:], r=[bQf], w=[bQu])
                        self.cp("dve", Qr[:], Qf[:], r=[bQf], w=[bQr])
                        self.rope(Qf[:].rearrange("p (h d) -> p h d", h=8), Qr[:].rearrange("p (h d) -> p h d", h=8), 8,
                                  cosT[:, t, :], sinT[:, t, :], tmp, btmp, [bQf, bcos, bsin], bQr)
                    elif gi == 1:
                        self.act(Kf[:], ps[:, 0:512], AF.Copy, r=[bps, bss], w=[bKf], scale=rstd)
                        self.cp("act", K1[:], Kf[:], r=[bKf], w=[bK1])
                        self.rope(Kf[:, 256:512].rearrange("p (h d) -> p h d", h=4),
                                  K1[:, 256:512].rearrange("p (h d) -> p h d", h=4), 4,
                                  cosT[:, t, :], sinT[:, t, :], tmp, btmp, [bKf, bcos, bsin], bK1)
                    elif gi in (2, 3, 4):
                        dst, bdst = ((Fq, bFq), (Fk, bFk), (Fv, bFv))[gi - 2]
                        self.act(dst[:], ps[:, 0:512], AF.Copy, r=[bps, bss], w=[bdst], scale=rstd)
                    else:
                        self.act(V2[:], ps[:, 0:256], AF.Copy, r=[bps, bss], w=[bV2], scale=rstd)
                        if 'G' not in DBG_SKIP:
                            self.ts("dve", gz[:, 0:24], ps[:, 256:280], rstd, None, ALU.mult, r=[bps, bss], w=[bgz])
                            self.tt("dve", gz[:, 0:24], gz[:, 0:24], bgate[:], ALU.add, r=[bgz, bbgate], w=[bgz])
                            self.ts("dve", gz[:, 32:40], ps[:, 280:288], rstd, None, ALU.mult, r=[bps, bss], w=[bgz])
                            self.tt("dve", gz[:, 32:40], gz[:, 32:40], bfor[:], ALU.add, r=[bgz, bbfor], w=[bgz])
                            self.act(gz[:, 0:40], gz[:, 0:40], AF.Exp, r=[bgz], w=[bgz], scale=-1.0)
                            self.ts("dve", gz[:, 0:40], gz[:, 0:40], 1.0, None, ALU.add, r=[bgz], w=[bgz])
                            self.act(gz[:, 32:40], gz[:, 32:40], AF.Ln, r=[bgz], w=[bgz])
                            self.recip(gz[:, 0:24], gz[:, 0:24], r=[bgz], w=[bgz])
                            self.cp("dve", gzb[:, 0:24], gz[:, 0:24], r=[bgz], w=[bgzb])
                            self.cp("dve", gzb[:, 32:40], gz[:, 32:40], r=[bgz], w=[bgzb])
                            self.tt("dve", r1t[:], gz[:, 32:40], gzb[:, 32:40], ALU.subtract, r=[bgz, bgzb], w=[br1t])
                            self.cp("dve", gzb[:, 64:72], r1t[:], r=[br1t], w=[bgzb])
                            self.tt("dve", r1t[:], r1t[:], gzb[:, 64:72], ALU.subtract, r=[br1t, bgzb], w=[br1t])
                            self.cp("dve", gzb[:, 96:104], r1t[:], r=[br1t], w=[bgzb])
                            pF = psF.bitcast(BF16)
                            self.tr(pF[:, 0:128], gzb[:], self.idb[:], r=[bgzb, self.b_idb], w=[bpsF])
                            self.cp("dve", GT[:, tok], pF[:, 0:128], r=[bpsF], w=[bGT])
                jobs = [(Qu, bQu, "QUT"), (Qr, bQr, "QRT"), (K1, bK1, "KVT"), (Fq, bFq, "FQT"), (Fk, bFk, "FKT")]
                if 'J' in DBG_SKIP:
                    jobs = []
                for (src, bsrc, dn) in jobs:
                    pt = psT2[t2i % 2]
                    bpt = bpsT2[t2i % 2]
                    t2i += 1
                    for j in range(4):
                        self.tr(pt[:, j * 128:(j + 1) * 128], src[:, j * 128:(j + 1) * 128], self.idb[:],
                                r=[bsrc, self.b_idb], w=[bpt])
                    st, bst = Tst.next()
                    self.cp("dve" if (t2i % 2) else "act", st[:].rearrange("p j q -> p (j q)"), pt[:, 0:512], r=[bpt], w=[bst])
                    if 'T' not in DBG_SKIP:
                        self.dma(D[dn].rearrange("(j p) q -> p j q", p=128)[:, :, tok], st[:], r=[bst], w=[self.db(dn, t)])
                if 'V' not in DBG_SKIP:
                    self.dma(D["FV"][tok, :], Fv[:], r=[bFv], w=[self.db("FV", t)])
                    self.dma(D["VS"][tok, :], V2[:, 0:128], r=[bV2], w=[self.db("VS", t)])
                    self.dma(D["VW"][tok, :], V2[:, 128:256], r=[bV2], w=[self.db("VW", t)])

    def rope(self, src, dst, nh, cos, sin, tmp, btmp, rd, bdst):
        if 'R' in DBG_SKIP:
            return
        cb = cos.unsqueeze(1).to_broadcast([128, nh, 8])
        sb_ = sin.unsqueeze(1).to_broadcast([128, nh, 8])
        x1 = src[:, :, 0:8]
        x2 = src[:, :, 8:16]
        t = tmp
        self.tt("dve", t[:, 0:nh, 0:8], x1, cb, ALU.mult, r=rd, w=[btmp])
        self.tt("dve", t[:, 0:nh, 8:16], x2, sb_, ALU.mult, r=rd, w=[btmp])
        self.tt("dve", t[:, 0:nh, 16:24], x1, sb_, ALU.mult, r=rd, w=[btmp])
        self.tt("dve", t[:, 0:nh, 24:32], x2, cb, ALU.mult, r=rd, w=[btmp])
        self.tt("dve", dst[:, :, 0:8], t[:, 0:nh, 0:8], t[:, 0:nh, 8:16], ALU.subtract, r=[btmp], w=[bdst])
        self.tt("dve", dst[:, :, 8:16], t[:, 0:nh, 16:24], t[:, 0:nh, 24:32], ALU.add, r=[btmp], w=[bdst])


def consts():
    bf = ml_dtypes.bfloat16
    pos = np.arange(SEQ, dtype=np.float32)
    inv = (500000.0 ** (-np.arange(8, dtype=np.float32) * 2.0 / 16)).astype(np.float32)
    ang = pos[:, None] * inv[None, :]
    c = {}
    tl = lambda a: np.ascontiguousarray(a.reshape(NT, 128, -1).transpose(1, 0, 2).reshape(128, -1))
    c["c_cos"] = tl(np.cos(ang).astype(np.float32))
    c["c_sin"] = tl(np.sin(ang).astype(np.float32))
    q = np.arange(SEQ)[:, None]
    j = np.arange(64)[None, :]
    cur = q // 64
    valid = (j * 64 <= q)
    forced = (j == 0) | (j == cur) | (j == cur - 1)
    c["c_valid"] = tl((valid & ~forced).astype(np.float32))
    c["c_add"] = tl(np.where(forced, 1e4, np.where(valid, 0.0, -1.0)).astype(np.float32))
    n_cmp = 255
    cs = np.arange(n_cmp)[:, None] * 16
    ss = np.arange(64)[None, :] * 64
    ov = np.clip(np.minimum(cs + 32, ss + 64) - np.maximum(cs, ss), 0, None) / 32.0
    imp = np.zeros((256, 65), np.float32)
    imp[:255, :64] = ov
    imp[:255, 64] = 1.0
    c["c_imp"] = imp.astype(bf)
    blk = (np.arange(SEQ)[None, :] // 64 == np.arange(64)[:, None]).astype(np.float32)
    c["c_blk"] = blk.astype(bf)
    c["c_idb"] = np.eye(128, dtype=np.float32).astype(bf)
    c["c_idf"] = np.eye(128, dtype=np.float32)
    oh = np.zeros((128, 24, 128), np.float32)
    for k in range(24):
        oh[k, k, :] = 1.0
    c["c_oh"] = oh.reshape(128, 24 * 128).astype(bf)
    kk = np.arange(128)[:, None]
    qq = np.arange(128)[None, :]
    tri = np.where(qq >= kk, 0.0, NEGM).astype(np.float32)
    c["c_tri"] = np.tile(tri, (1, 4)).astype(bf)
    tric = np.where(kk > qq, 0.0, NEGM).astype(np.float32)
    c["c_tric"] = np.tile(tric, (1, 4)).astype(bf)
    cm = np.zeros((128, 17, 4, 128), np.float32)
    for i in range(17):
        ok = (128 * i + qq - 16 * kk - 31) >= 0
        cm[:, i, :, :] = np.where(ok, 0.0, NEGM)[:, None, :]
    c["c_cmpm"] = cm.reshape(128, 17 * 512).astype(bf)
    return c


_CACHE = {}


def kernel(**inputs):
    if "nc" not in _CACHE:
        k = K()
        _CACHE["nc"] = k.build()
    nc = _CACHE["nc"]
    cst = consts()
    shared = {n: np.ascontiguousarray(v) for n, v in inputs.items() if n != "x"}
    shared.update(cst)
    x = np.ascontiguousarray(inputs["x"])
    in_maps = []
    for b in range(8):
        m = dict(shared)
        m["x"] = x[b]
        in_maps.append(m)
    res = run_bass_kernel_spmd(nc, in_maps, core_ids=list(range(8)))
    return np.stack([r["out"] for r in res.results], axis=0).astype(np.float32)


def _phase_f(self, l, xin, xin_n):
    nc = self.nc
    I, D = self.I, self.D
    import contextlib
    with contextlib.ExitStack() as es:
        def sb(shape, dtp, name):
            self.sb_n += 1
            nm = "%s_%d" % (name, self.sb_n)
            t = es.enter_context(nc.sbuf_tensor(nm, list(shape), dtp))
            return t, Buf(nm)

        def rot(n, shape, dtp, name):
            return Rot([sb(shape, dtp, name) for _ in range(n)])
        W, bW = sb([128, 8, DM], BF16, "Wout")
        stg = rot(2, [128, DM], F32, "wstg")
        gpost, bgpost = sb([128, DM], F32, "gpost")
        self.dma(gpost[:], I["g_post_mix"][l:l + 1, :].partition_broadcast(128), w=[bgpost])
        for c in range(8):
            st, bst = stg.next()
            self.dma(st[:], I["w_out"][l, c * 128:(c + 1) * 128, :], w=[bst])
            self.cp("dve" if c % 2 else "act", W[:, c, :], st[:], r=[bst], w=[bW])
        mts = rot(2, [128, 8, 128], BF16, "mixT")
        xts = rot(2, [128, DM], F32, "xt")
        ts_ = rot(2, [128, DM], F32, "tt")
        sss = rot(2, [128, 4], F32, "ss")
        junk, bjunk = sb([128, DM], BF16, "junk")
        PS = [self.PA, self.PB]
        bPS = [self.bPA, self.bPB]
        for t in range(NT if NT_LIMIT is None else NT_LIMIT):
            tok = slice(t * 128, (t + 1) * 128)
            mt, bmt = mts.next()
            xt, bxt = xts.next()
            tt, btt = ts_.next()
            ss, bss = sss.next()
            ps, bps = PS[t % 2], bPS[t % 2]
            self.dma(mt[:], D["MIXT"].rearrange("(c p) q -> p c q", p=128)[:, :, tok], r=[self.db("MIXT", t)], w=[bmt])
            self.dma(xt[:], xin[tok, :], r=[self.db(xin_n, t)], w=[bxt])
            for half in range(2):
                for c in range(8):
                    self.mm(ps[:, half * 512:(half + 1) * 512], mt[:, c, :], W[:, c, half * 512:(half + 1) * 512],
                            start=(c == 0), stop=(c == 7), r=[bmt, bW], w=[bps])
            self.act(junk[:, 0:512], ps[:, 0:512], AF.Square, r=[bps], w=[bjunk, bss], accum_out=ss[:, 0:1])
            self.act(junk[:, 512:1024], ps[:, 512:1024], AF.Square, r=[bps], w=[bjunk, bss], accum_out=ss[:, 2:3])
            self.tt("dve", ss[:, 0:1], ss[:, 0:1], ss[:, 2:3], ALU.add, r=[bss], w=[bss])
            self.rstd_from_ss(ss[:, 0:1], bss, ss[:, 1:2], bss)
            self.tt("dve", tt[:, 0:512], ps[:, 0:512], gpost[:, 0:512], ALU.mult, r=[bps, bgpost], w=[btt])
            self.tt("dve", tt[:, 512:1024], ps[:, 512:1024], gpost[:, 512:1024], ALU.mult, r=[bps, bgpost], w=[btt])
            self.stt(tt[:], tt[:], ss[:, 1:2], xt[:], ALU.mult, ALU.add, r=[btt, bss, bxt], w=[btt])
            self.dma(D["XMID"][tok, :], tt[:], r=[btt], w=[self.db("XMID", t)])


def _phase_g(self, l, xout, xout_n):
    nc = self.nc
    I, D = self.I, self.D
    import contextlib
    with contextlib.ExitStack() as es:
        def sb(shape, dtp, name):
            self.sb_n += 1
            nm = "%s_%d" % (name, self.sb_n)
            t = es.enter_context(nc.sbuf_tensor(nm, list(shape), dtp))
            return t, Buf(nm)

        def rot(n, shape, dtp, name):
            return Rot([sb(shape, dtp, name) for _ in range(n)])
        Wu, bWu = sb([128, 8, DFF], BF16, "Wup")
        Wd, bWd = sb([128, 32, DM], BF16, "Wdn")
        stg = rot(2, [128, 2048], F32, "wstg")
        gpre, bgpre = sb([128, 8], F32, "gpre")
        gpost, bgpost = sb([128, DM], F32, "gpost")
        self.dma(gpre[:], I["g_pre_mlp"][l].rearrange("(c p) -> p c", p=128), w=[bgpre])
        self.dma(gpost[:], I["g_post_mlp"][l:l + 1, :].partition_broadcast(128), w=[bgpost])
        k = 0
        for c in range(8):
            for hf in range(2):
                st, bst = stg.next()
                self.dma(st[:], I["w_up"][l, c * 128:(c + 1) * 128, hf * 2048:(hf + 1) * 2048], w=[bst])
                if k % 2:
                    self.ts("dve", Wu[:, c, hf * 2048:(hf + 1) * 2048], st[:], gpre[:, c:c + 1], None, ALU.mult, r=[bst, bgpre], w=[bWu])
                else:
                    self.act(Wu[:, c, hf * 2048:(hf + 1) * 2048], st[:], AF.Copy, r=[bst, bgpre], w=[bWu], scale=gpre[:, c:c + 1])
                k += 1
        for c in range(0, 32, 2):
            st, bst = stg.next()
            self.dma(st[:].rearrange("p (a n) -> p a n", a=2), I["w_down"][l, c * 128:(c + 2) * 128, :].rearrange("(a p) n -> p a n", p=128), w=[bst])
            self.cp("dve" if k % 2 else "act", Wd[:, c:c + 2, :].rearrange("p a n -> p (a n)"), st[:], r=[bst], w=[bWd])
            k += 1
        xts = rot(2, [128, 2, DM], F32, "xt")
        xbs = rot(2, [128, 2, DM], BF16, "xb")
        hTs = rot(2, [128, 8, 256], BF16, "hT")
        rs_ = rot(3, [128, 256], BF16, "r")
        as_ = rot(3, [128, 256], BF16, "a")
        ts_ = rot(2, [128, DM], F32, "tt")
        sss = rot(2, [128, 8], F32, "ss")
        junk, bjunk = sb([128, DM], BF16, "junk")
        psT = self.PQ[2][:].bitcast(BF16)
        bpsT = self.bPQ[2]
        Y = [self.PA, self.PB]
        bY = [self.bPA, self.bPB]
        U = [self.PQ[0], self.PQ[1]]
        bU = [self.bPQ[0], self.bPQ[1]]
        ntl = 16 if NT_LIMIT is None else max(1, NT_LIMIT // 2)
        for t in range(ntl):
            xt, bxt = xts.next()
            xb, bxb = xbs.next()
            hT, bhT = hTs.next()
            ss, bss = sss.next()
            self.dma(xt[:], D["XMID"][t * 256:(t + 1) * 256, :].rearrange("(a p) n -> p a n", p=128),
                     r=[self.db("XMID", 2 * t), self.db("XMID", 2 * t + 1)], w=[bxt])
            for a in range(2):
                self.act(junk[:], xt[:, a, :], AF.Square, r=[bxt], w=[bjunk, bss], accum_out=ss[:, 2 * a:2 * a + 1])
                self.rstd_from_ss(ss[:, 2 * a:2 * a + 1], bss, ss[:, 2 * a + 1:2 * a + 2], bss)
                self.ts("dve", xb[:, a, :], xt[:, a, :], ss[:, 2 * a + 1:2 * a + 2], None, ALU.mult, r=[bxt, bss], w=[bxb])
                for c in range(8):
                    self.tr(psT[:, c * 128:(c + 1) * 128], xb[:, a, c * 128:(c + 1) * 128], self.idb[:],
                            r=[bxb, self.b_idb], w=[bpsT])
                self.cp("act", hT[:, :, a * 128:(a + 1) * 128], psT[:, :].rearrange("p (c q) -> p c q", c=8), r=[bpsT], w=[bhT])
            for fc in range(32):
                u, bu = U[fc % 2], bU[fc % 2]
                for c in range(8):
                    self.mm(u[:, 0:256], Wu[:, c, fc * 128:(fc + 1) * 128], hT[:, c, :], start=(c == 0), stop=(c == 7),
                            r=[bWu, bhT], w=[bu])
                r_, br_ = rs_.next()
                a_, ba_ = as_.next()
                self.act(r_[:], u[:, 0:256], AF.Relu, r=[bu], w=[br_])
                self.tt("dve", a_[:], u[:, 0:256], r_[:], ALU.mult, r=[bu, br_], w=[ba_])
                for a in range(2):
                    for half in range(2):
                        self.mm(Y[a][:, half * 512:(half + 1) * 512], a_[:, a * 128:(a + 1) * 128],
                                Wd[:, fc, half * 512:(half + 1) * 512], start=(fc == 0), stop=(fc == 31),
                                r=[ba_, bWd], w=[bY[a]])
            for a in range(2):
                tt, btt = ts_.next()
                tok = slice(t * 256 + a * 128, t * 256 + (a + 1) * 128)
                self.act(junk[:, 0:512], Y[a][:, 0:512], AF.Square, r=[bY[a]], w=[bjunk, bss], accum_out=ss[:, 2 * a:2 * a + 1])
                self.act(junk[:, 512:1024], Y[a][:, 512:1024], AF.Square, r=[bY[a]], w=[bjunk, bss], accum_out=ss[:, 4 + a:5 + a])
                self.tt("dve", ss[:, 2 * a:2 * a + 1], ss[:, 2 * a:2 * a + 1], ss[:, 4 + a:5 + a], ALU.add, r=[bss], w=[bss])
                self.rstd_from_ss(ss[:, 2 * a:2 * a + 1], bss, ss[:, 2 * a + 1:2 * a + 2], bss)
                self.tt("dve", tt[:, 0:512], Y[a][:, 0:512], gpost[:, 0:512], ALU.mult, r=[bY[a], bgpost], w=[btt])
                self.tt("dve", tt[:, 512:1024], Y[a][:, 512:1024], gpost[:, 512:1024], ALU.mult, r=[bY[a], bgpost], w=[btt])
                self.stt(tt[:], tt[:], ss[:, 2 * a + 1:2 * a + 2], xt[:, a, :], ALU.mult, ALU.add, r=[btt, bss, bxt], w=[btt])
                self.dma(xout[tok, :], tt[:], r=[btt], w=[self.db(xout_n, 2 * t + a)])


K.phase_f = _phase_f
K.phase_g = _phase_g


def _phase_b(self, l, LFT, bLFT):
    nc = self.nc
    I, D = self.I, self.D
    import contextlib
    with contextlib.ExitStack() as es:
        def sb(shape, dtp, name):
            self.sb_n += 1
            nm = "%s_%d" % (name, self.sb_n)
            t = es.enter_context(nc.sbuf_tensor(nm, list(shape), dtp))
            return t, Buf(nm)
        ones, bones = sb([8, SEQ], F32, "ones")
        c8, bc8 = sb([8, SEQ], F32, "c8")
        rr, brr = sb([8, SEQ], F32, "rr")
        fq, bfq = sb([8, 4, SEQ], BF16, "fqa")
        fk, bfk = sb([8, 4, SEQ], BF16, "fka")
        self.memset("dve", ones[:], 1.0, w=[bones])
        self.S.op("dve", lambda: nc.vector.tensor_tensor_scan(out=c8[:], data0=ones[:], data1=LFT[:], initial=0.0,
                                                               op0=ALU.mult, op1=ALU.add), [bones, bLFT], [bc8])
        self.ts("dve", c8[:], c8[:], 8.0, None, ALU.mult, r=[bc8], w=[bc8])
        self.cp("dve", fq[:, 0, :], c8[:], r=[bc8], w=[bfq])
        for j in (1, 2, 3):
            self.cp("dve", fq[:, j, :], ones[:], r=[bones], w=[bfq])
        self.cp("dve", fk[:, 0, :], ones[:], r=[bones], w=[bfk])
        self.ts("dve", c8[:], c8[:], -1.0, None, ALU.mult, r=[bc8], w=[bc8])
        self.cp("dve", fk[:, 1, :], c8[:], r=[bc8], w=[bfk])
        self.tt("dve", rr[:], c8[:], fk[:, 1, :], ALU.subtract, r=[bc8, bfk], w=[brr])
        self.cp("dve", fk[:, 2, :], rr[:], r=[brr], w=[bfk])
        self.tt("dve", rr[:], rr[:], fk[:, 2, :], ALU.subtract, r=[brr, bfk], w=[brr])
        self.cp("dve", fk[:, 3, :], rr[:], r=[brr], w=[bfk])
        self.dma(D["FQA"][:, :, :], fq[:], r=[bfq], w=[self.db("FQA")])
        self.dma(D["FKA"][:, :, :], fk[:], r=[bfk], w=[self.db("FKA")])


def _phase_fox(self, l):
    nc = self.nc
    I, D = self.I, self.D
    import contextlib
    with contextlib.ExitStack() as es:
        def sb(shape, dtp, name):
            self.sb_n += 1
            nm = "%s_%d" % (name, self.sb_n)
            t = es.enter_context(nc.sbuf_tensor(nm, list(shape), dtp))
            return t, Buf(nm)

        def rot(n, shape, dtp, name):
            return Rot([sb(shape, dtp, name) for _ in range(n)])
        KTs = rot(2, [68, SEQ], BF16, "fKT")
        VTs = rot(2, [128, NT, 128], BF16, "fVT")
        for (vt, bvt) in VTs.items:
            self.memset("dve", vt[:, :, 64:128], 1.0, w=[bvt])
        Qs = rot(2, [68, 512], BF16, "fQ")
        PTs = rot(3, [128, 512], BF16, "fPT")
        Rs = rot(2, [64, 512], F32, "fR")
        Os = rot(2, [64, 512], BF16, "fO")
        tri, btri = sb([128, 512], BF16, "tri")
        self.dma(tri[:], I["c_tri"][:, :], w=[btri])
        nh = 8 if NT_LIMIT is None else 1
        nI = 8 if NT_LIMIT is None else max(1, NT_LIMIT // 4)
        oi = 0
        si = 0
        for h in range(nh):
            KT, bKT = KTs.next()
            VT, bVT = VTs.next()
            self.dma(KT[0:64, :], D["FKT"][h * 64:(h + 1) * 64, :], r=self.dbs("FKT", 0, NT), w=[bKT])
            self.dma(KT[64:68, :], D["FKA"][h, :, :], r=[self.db("FKA")], w=[bKT])
            self.dma(VT[:, :, 0:64], D["FV"].rearrange("(t p) (h d) -> p t h d", p=128, h=8)[:, :, h, :],
                     r=self.dbs("FV", 0, NT), w=[bVT])
            for Iq in range(nI):
                Q, bQ = Qs.next()
                qs = slice(Iq * 512, (Iq + 1) * 512)
                self.dma(Q[0:64, :], D["FQT"][h * 64:(h + 1) * 64, qs], r=self.dbs("FQT", 4 * Iq, 4 * Iq + 4), w=[bQ])
                self.dma(Q[64:68, :], D["FQA"][h, :, qs], r=[self.db("FQA")], w=[bQ])
                Oacc, bO = self.PQ[2 + oi % 2], self.bPQ[2 + oi % 2]
                oi += 1
                nkt = 4 * Iq + 4
                for kt in range(nkt):
                    m = kt - 4 * Iq
                    c0 = 128 * m if m > 0 else 0
                    Sp, bS = self.PQ[si % 2], self.bPQ[si % 2]
                    si += 1
                    PT, bPT = PTs.next()
                    self.mm(Sp[:, c0:512], KT[0:68, kt * 128:(kt + 1) * 128], Q[0:68, c0:512], start=True, stop=(m < 0),
                            r=[bKT, bQ], w=[bS])
                    if m >= 0:
                        self.mm(Sp[:, c0:c0 + 128], self.idb[:], tri[:, 0:128], start=False, stop=True,
                                r=[self.b_idb, btri], w=[bS])
                    self.act(PT[:, c0:512], Sp[:, c0:512], AF.Exp, r=[bS], w=[bPT], scale=0.125)
                    self.mm(Oacc[:, c0:512], VT[:, kt, :], PT[:, c0:512], start=(kt == 0), stop=(kt == nkt - 1),
                            r=[bVT, bPT], w=[bO])
                R, bR = Rs.next()
                O, bOs = Os.next()
                self.recip(R[:], Oacc[64:128, :], r=[bO], w=[bR])
                self.tt("dve", O[:], Oacc[0:64, :], R[:], ALU.mult, r=[bO, bR], w=[bOs])
                self.dma(D["MIXT"][512 + h * 64:512 + (h + 1) * 64, qs], O[:], r=[bOs],
                         w=[self.db("MIXT", 4 * Iq + j) for j in range(4)])


K.phase_b = _phase_b
K.phase_fox = _phase_fox


def _phase_nsa(self, l, GT, bGT):
    nc = self.nc
    I, D = self.I, self.D
    import contextlib
    with contextlib.ExitStack() as es:
        def sb(shape, dtp, name):
            self.sb_n += 1
            nm = "%s_%d" % (name, self.sb_n)
            t = es.enter_context(nc.sbuf_tensor(nm, list(shape), dtp))
            return t, Buf(nm)

        def rot(n, shape, dtp, name):
            return Rot([sb(shape, dtp, name) for _ in range(n)])
        KCMPT, bKC = sb([128, 2, 256], BF16, "KCMPT")
        VCMP, bVC = sb([128, 2, 2, 128], BF16, "VCMP")
        self.memset("dve", KCMPT[:], 0.0, w=[bKC])
        self.memset("dve", VCMP[:], 0.0, w=[bVC])
        self.memset("dve", VCMP[:, :, :, 64:128], 1.0, w=[bVC])
        with contextlib.ExitStack() as es2:
            def sb2(shape, dtp, name):
                self.sb_n += 1
                nm = "%s_%d" % (name, self.sb_n)
                t = es2.enter_context(nc.sbuf_tensor(nm, list(shape), dtp))
                return t, Buf(nm)
            kcT, bkcT = sb2([64, 4, SEQ], BF16, "kcT")
            kcp = [sb2([128, 4, SEQ], BF16, "kcp%d" % r) for r in range(2)]
            for r in range(2):
                self.memset("dve", kcp[r][0][64:128, :, :], 0.0, w=[kcp[r][1]])
            posT, bpos = sb2([64, 2, 32], F32, "posT")
            w1s, bw1s = sb2([64, 16, 256], F32, "w1s")
            w1b = [sb2([128, 32, 256], BF16, "w1b%d" % i) for i in range(2)]
            for i in range(2):
                self.memset("dve", w1b[i][0][64:128, :, :], 0.0, w=[w1b[i][1]])
            b1T, bb1 = sb2([128, 2, 2], F32, "b1T")
            w2s, bw2s = sb2([128, 2, 2, 64], F32, "w2s")
            w2b, bw2b = sb2([128, 2, 2, 128], BF16, "w2b")
            self.memset("dve", w2b[:], 0.0, w=[bw2b])
            b2k, bb2k = sb2([64, 1], F32, "b2k")
            b2v, bb2v = sb2([128, 64], F32, "b2v")
            hid, bhid = sb2([128, 2, 256], BF16, "hid")
            xs, bxs = sb2([128, 256], F32, "xs")
            uu, buu = sb2([128, 256], F32, "uu")
            self.dma(kcT[:], D["KVT"][0:256, :].rearrange("(j d) q -> d j q", d=64), r=self.dbs("KVT", 0, NT), w=[bkcT])
            for kv, nm in enumerate(("k", "v")):
                self.dma(posT[:, kv, :], I["cmp_pos_" + nm][l].rearrange("t d -> d t"), w=[bpos])
                self.dma(b1T[:, kv, :], I["cmp_b1_" + nm][l].rearrange("(c p) -> p c", p=128), w=[bb1])
                self.dma(w2s[:, kv, :, :], I["cmp_w2_" + nm][l].rearrange("(c p) d -> p c d", p=128), w=[bw2s])
            self.cp("dve", w2b[:, :, :, 0:64], w2s[:], r=[bw2s], w=[bw2b])
            self.dma(b2k[:], I["cmp_b2_k"][l].rearrange("(d o) -> d o", o=1), w=[bb2k])
            self.dma(b2v[:], I["cmp_b2_v"][l:l + 1, :].partition_broadcast(128), w=[bb2v])
            for kv, nm in enumerate(("k", "v")):
                for hf in range(2):
                    self.dma(w1s[:], I["cmp_w1_" + nm][l].rearrange("(t d) h -> d t h", d=64)[:, hf * 16:(hf + 1) * 16, :], w=[bw1s])
                    self.cp("dve", w1b[kv][0][0:64, hf * 16:(hf + 1) * 16, :], w1s[:], r=[bw1s], w=[w1b[kv][1]])
            for r_ in range(2):
                for j in range(4):
                    self.tt("dve", kcp[r_][0][0:64, j, :].rearrange("d (c i) -> d c i", i=16),
                            kcT[:, j, :].rearrange("d (c i) -> d c i", i=16),
                            posT[:, j // 2, 16 * r_:16 * r_ + 16].unsqueeze(1).to_broadcast([64, 256, 16]), ALU.add,
                            r=[bkcT, bpos], w=[kcp[r_][1]])
            pi = 0
            for kv in range(2):
                for g in range(2):
                    j = kv * 2 + g
                    for hc in range(2):
                        ps, bps = self.PQ[pi % 2], self.bPQ[pi % 2]
                        pi += 1
                        for tau in range(32):
                            r_, jj = tau // 16, tau % 16
                            rhs = kcp[r_][0][:, j, :].rearrange("d (c i) -> d c i", i=16)[:, r_:r_ + 255, jj]
                            self.mm(ps[:, 0:255], w1b[kv][0][:, tau, hc * 128:(hc + 1) * 128], rhs, start=(tau == 0),
                                    stop=(tau == 31), r=[w1b[kv][1], kcp[r_][1]], w=[bps])
                        self.act(xs[:, 0:255], ps[:, 0:255], AF.Identity, r=[bps, bb1], w=[bxs], bias=b1T[:, kv, hc:hc + 1])
                        self.tt("dve", uu[:, 0:255], xs[:, 0:255], xs[:, 0:255], ALU.mult, r=[bxs], w=[buu])
                        self.ts("dve", uu[:, 0:255], uu[:, 0:255], 0.044715, 1.0, ALU.mult, ALU.add, r=[buu], w=[buu])
                        self.tt("dve", uu[:, 0:255], uu[:, 0:255], xs[:, 0:255], ALU.mult, r=[buu, bxs], w=[buu])
                        self.act(uu[:, 0:255], uu[:, 0:255], AF.Exp, r=[buu], w=[buu], scale=-1.5957691216057308)
                        self.ts("dve", uu[:, 0:255], uu[:, 0:255], 1.0, None, ALU.add, r=[buu], w=[buu])
                        self.recip(uu[:, 0:255], uu[:, 0:255], r=[buu], w=[buu])
                        self.tt("dve", hid[:, hc, 0:255], xs[:, 0:255], uu[:, 0:255], ALU.mult, r=[bxs, buu], w=[bhid])
                    if kv == 0:
                        ps, bps = self.PQ[2], self.bPQ[2]
                        for hc in range(2):
                            self.mm(ps[:, 0:255], w2b[:, 0, hc, :], hid[:, hc, 0:255], start=(hc == 0), stop=(hc == 1),
                                    r=[bw2b, bhid], w=[bps])
                        self.act(KCMPT[0:64, g, 0:255], ps[0:64, 0:255], AF.Identity, r=[bps, bb2k], w=[bKC], bias=b2k[:, 0:1])
                    else:
                        for m in range(2):
                            nn = 128 if m == 0 else 127
                            ps, bps = self.PQ[2 + m], self.bPQ[2 + m]
                            for hc in range(2):
                                self.mm(ps[0:nn, 0:64], hid[:, hc, m * 128:m * 128 + nn], w2b[:, 1, hc, 0:64], start=(hc == 0),
                                        stop=(hc == 1), r=[bhid, bw2b], w=[bps])
                            self.tt("dve", VCMP[0:nn, g, m, 0:64], ps[0:nn, 0:64], b2v[0:nn, :], ALU.add, r=[bps, bb2v], w=[bVC])
        self.S.barrier()
        KSEL, bKS = sb([128, 2, SEQ], BF16, "KSEL")
        VSEL, bVS = sb([128, NT, 2, 128], BF16, "VSEL")
        KWIN, bKW = sb([128, 2, SEQ], BF16, "KWIN")
        self.memset("dve", KWIN[64:128, :, :], 0.0, w=[bKW])
        VWIN, bVW = sb([128, NT, 2, 128], BF16, "VWIN")
        self.dma(KSEL[0:64, :, :], D["KVT"][256:384, :].rearrange("(g d) q -> d g q", d=64), r=self.dbs("KVT", 0, NT), w=[bKS])
        for g in range(2):
            self.dma(KSEL[64:128, g, :], I["c_blk"][:, :], w=[bKS])
        self.dma(KWIN[0:64, :, :], D["KVT"][384:512, :].rearrange("(g d) q -> d g q", d=64), r=self.dbs("KVT", 0, NT), w=[bKW])
        self.memset("dve", VSEL[:, :, :, 64:128], 1.0, w=[bVS])
        self.memset("dve", VWIN[:, :, :, 64:128], 1.0, w=[bVW])
        for g in range(2):
            self.dma(VSEL[:, :, g, 0:64], D["VS"].rearrange("(t p) (g d) -> p t g d", p=128, g=2)[:, :, g, :], r=self.dbs("VS", 0, NT), w=[bVS])
            self.dma(VWIN[:, :, g, 0:64], D["VW"].rearrange("(t p) (g d) -> p t g d", p=128, g=2)[:, :, g, :], r=self.dbs("VW", 0, NT), w=[bVW])
        tri, btri = sb([128, 512], BF16, "tri")
        tric, btric = sb([128, 512], BF16, "tric")
        cmpm, bcmpm = sb([128, 17, 512], BF16, "cmpm")
        impm, bimpm = sb([128, 2, 65], BF16, "impm")
        oh, boh = sb([128, 24, 128], BF16, "oh")
        vld, bvld = sb([128, NT, 64], F32, "vld")
        addt, baddt = sb([128, NT, 64], F32, "addt")
        self.dma(tri[:], I["c_tri"][:, :], w=[btri])
        self.dma(tric[:], I["c_tric"][:, :], w=[btric])
        self.dma(cmpm[:].rearrange("p a b -> p (a b)"), I["c_cmpm"][:, :], w=[bcmpm])
        self.dma(impm[:], I["c_imp"].rearrange("(m p) c -> p m c", p=128), w=[bimpm])
        self.dma(oh[:].rearrange("p a b -> p (a b)"), I["c_oh"][:, :], w=[boh])
        self.dma(vld[:].rearrange("p a b -> p (a b)"), I["c_valid"][:, :], w=[bvld])
        self.dma(addt[:].rearrange("p a b -> p (a b)"), I["c_add"][:, :], w=[baddt])
        Qus = rot(2, [128, 4, 128], BF16, "nQu")
        for (q_, bq_) in Qus.items:
            self.memset("dve", q_[64:128, :, :], 0.0, w=[bq_])
        Qas = rot(2, [128, 4, 128], BF16, "nQa")
        PTs = rot(3, [128, 512], BF16, "nPT")
        imps = rot(2, [128, 64], F32, "imp")
        rs4, brs4 = sb([128, 4], F32, "rs4")
        m8, bm8 = sb([128, 16], F32, "m8")
        scr, bscr = sb([128, 64], F32, "scr")
        mneg, bmneg = sb([128, 128], BF16, "mneg")
        self.memset("dve", mneg[:], 0.0, w=[bmneg])
        Rt, bRt = sb([64, 512], F32, "Rt")
        Ct, bCt = sb([64, 512], F32, "Ct")
        acc, bacc = sb([64, 512], F32, "acc")
        mst = rot(2, [128, 2, 128], BF16, "mst")
        S_ps = [(self.PQ[0], self.bPQ[0]), (self.PQ[1], self.bPQ[1])]
        O_ps = [(self.PA[:, 0:512], self.bPAh[0]), (self.PA[:, 512:1024], self.bPAh[1])]
        IMP, bIMP = self.PQ[2], self.bPQ[2]
        GB, bGB = self.PQ[3], self.bPQ[3]
        TP, bTP = self.PB[:, 0:512].bitcast(BF16), self.bPBh[0]
        si = 0
        oi = 0
        nI = NT if NT_LIMIT is None else NT_LIMIT
        for i in range(nI):
            qs = slice(i * 128, (i + 1) * 128)
            for g in range(2):
                Qu, bQu = Qus.next()
                Qa, bQa = Qas.next()
                ms, bms = mst.next()
                self.dma(Qu[0:64, :, :], D["QUT"][g * 256:(g + 1) * 256, qs].rearrange("(h d) q -> d h q", d=64), r=[self.db("QUT", i)], w=[bQu])
                self.dma(Qa[0:64, :, :], D["QRT"][g * 256:(g + 1) * 256, qs].rearrange("(h d) q -> d h q", d=64), r=[self.db("QRT", i)], w=[bQa])
                branches = []
                nm_ = 2 if i >= 16 else 1
                Oc, bOc = O_ps[oi % 2]
                oi += 1
                for m in range(nm_):
                    Sp, bS = S_ps[si % 2]
                    si += 1
                    PT, bPT = PTs.next()
                    mi = i if m == 0 else i - 16
                    need_mask = (m * 128 + 127) > (8 * i - 2)
                    self.mm(Sp[:, :], KCMPT[:, g, m * 128:(m + 1) * 128], Qu[:].rearrange("d h q -> d (h q)"), start=True,
                            stop=(not need_mask), r=[bKC, bQu], w=[bS])
                    if need_mask:
                        self.mm(Sp[:, :], self.idb[:], cmpm[:, min(mi, 16), :], start=False, stop=True, r=[self.b_idb, bcmpm], w=[bS])
                    self.act(PT[:], Sp[:, :], AF.Exp, r=[bS], w=[bPT], scale=0.125)
                    self.mm(Oc, VCMP[:, g, m, :], PT[:], start=(m == 0), stop=(m == nm_ - 1), r=[bVC, bPT], w=[bOc])
                    for hh in range(4):
                        self.mm(IMP[:, hh * 65:(hh + 1) * 65], PT[:, hh * 128:(hh + 1) * 128], impm[:, m, :],
                                start=(m == 0 and hh == 0), stop=(m == nm_ - 1), r=[bPT, bimpm], w=[bIMP])
                branches.append((0, Oc, bOc))
                imp, bimp = imps.next()
                IMPv = IMP[:, 0:260].rearrange("p (h c) -> p h c", c=65)
                self.ts("dve", rs4[:], IMPv[:, :, 64], 1e-30, None, ALU.max, r=[bIMP], w=[brs4])
                self.recip(rs4[:], rs4[:], r=[brs4], w=[brs4])
                self.ts("dve", imp[:], IMPv[:, 0, 0:64], rs4[:, 0:1], None, ALU.mult, r=[bIMP, brs4], w=[bimp])
                for hh in range(1, 4):
                    self.ts("dve", scr[:], IMPv[:, hh, 0:64], rs4[:, hh:hh + 1], None, ALU.mult, r=[bIMP, brs4], w=[bscr])
                    self.tt("dve", imp[:], imp[:], scr[:], ALU.add, r=[bimp, bscr], w=[bimp])
                self.tt("dve", imp[:], imp[:], vld[:, i, :], ALU.mult, r=[bimp, bvld], w=[bimp])
                self.tt("dve", imp[:], imp[:], addt[:, i, :], ALU.add, r=[bimp, baddt], w=[bimp])
                self.S.op("dve", (lambda a=m8, b=imp: nc.vector.max(out=a[:, 0:8], in_=b[:])), [bimp], [bm8])
                self.S.op("dve", (lambda a=m8, b=imp, c=scr: nc.vector.match_replace(out=c[:], in_to_replace=a[:, 0:8], in_values=b[:], imm_value=-1e30)), [bimp, bm8], [bscr])
                self.S.op("dve", (lambda a=m8, c=scr: nc.vector.max(out=a[:, 8:16], in_=c[:])), [bscr], [bm8])
                self.ts("dve", scr[:], imp[:], m8[:, 15:16], None, ALU.is_ge, r=[bimp, bm8], w=[bscr])
                self.ts("dve", mneg[:, 64:128], scr[:], -NEGM, NEGM, ALU.mult, ALU.add, r=[bscr], w=[bmneg])
                self.tr(TP[:, 0:128], mneg[:], self.idb[:], r=[bmneg, self.b_idb], w=[bTP])
                for hh in range(4):
                    self.cp("dve" if hh % 2 else "act", Qa[64:128, hh, :], TP[64:128, 0:128], r=[bTP], w=[bQa])
                Os_, bOs_ = O_ps[oi % 2]
                oi += 1
                for kt in range(i + 1):
                    Sp, bS = S_ps[si % 2]
                    si += 1
                    PT, bPT = PTs.next()
                    self.mm(Sp[:, :], KSEL[:, g, kt * 128:(kt + 1) * 128], Qa[:].rearrange("d h q -> d (h q)"), start=True,
                            stop=(kt != i), r=[bKS, bQa], w=[bS])
                    if kt == i:
                        self.mm(Sp[:, :], self.idb[:], tri[:], start=False, stop=True, r=[self.b_idb, btri], w=[bS])
                    self.act(PT[:], Sp[:, :], AF.Exp, r=[bS], w=[bPT], scale=0.125)
                    self.mm(Os_, VSEL[:, kt, g, :], PT[:], start=(kt == 0), stop=(kt == i), r=[bVS, bPT], w=[bOs_])
                branches.append((1, Os_, bOs_))
                def combine(br, Op, bOp, first, last):
                    for hh in range(4):
                        idx = (g * 4 + hh) * 3 + br
                        self.mm(GB[:, hh * 128:(hh + 1) * 128], oh[:, idx, :], GT[:, qs], start=True, stop=True,
                                r=[boh, bGT], w=[bGB])
                    if br == 0:
                        self.ts("dve", Rt[:], Op[64:128, :], 1e-30, None, ALU.max, r=[bOp], w=[bRt])
                        self.recip(Rt[:], Rt[:], r=[bRt], w=[bRt])
                    else:
                        self.recip(Rt[:], Op[64:128, :], r=[bOp], w=[bRt])
                    self.tt("dve", Ct[:], GB[0:64, :], Rt[:], ALU.mult, r=[bGB, bRt], w=[bCt])
                    if first:
                        self.tt("dve", acc[:], Op[0:64, :], Ct[:], ALU.mult, r=[bOp, bCt], w=[bacc])
                    else:
                        self.tt("dve", Ct[:], Op[0:64, :], Ct[:], ALU.mult, r=[bOp, bCt], w=[bCt])
                        if not last:
                            self.tt("dve", acc[:], acc[:], Ct[:], ALU.add, r=[bacc, bCt], w=[bacc])
                        else:
                            a3 = acc[:].rearrange("d (h q) -> d h q", h=4)
                            c3 = Ct[:].rearrange("d (h q) -> d h q", h=4)
                            self.tt("dve", ms[0:64, :, :], a3[:, 0::2, :], c3[:, 0::2, :], ALU.add, r=[bacc, bCt], w=[bms])
                            self.tt("dve", ms[64:128, :, :], a3[:, 1::2, :], c3[:, 1::2, :], ALU.add, r=[bacc, bCt], w=[bms])
                combine(0, Oc, bOc, True, False)
                Ow, bOw = O_ps[oi % 2]
                oi += 1
                k0 = max(0, i - 4)
                for kt in range(k0, i + 1):
                    Sp, bS = S_ps[si % 2]
                    si += 1
                    PT, bPT = PTs.next()
                    masked = (kt == i) or (kt == i - 4)
                    self.mm(Sp[:, :], KWIN[:, g, kt * 128:(kt + 1) * 128], Qa[:].rearrange("d h q -> d (h q)"), start=True,
                            stop=(not masked), r=[bKW, bQa], w=[bS])
                    if kt == i:
                        self.mm(Sp[:, :], self.idb[:], tri[:], start=False, stop=True, r=[self.b_idb, btri], w=[bS])
                    elif kt == i - 4:
                        self.mm(Sp[:, :], self.idb[:], tric[:], start=False, stop=True, r=[self.b_idb, btric], w=[bS])
                    self.act(PT[:], Sp[:, :], AF.Exp, r=[bS], w=[bPT], scale=0.125)
                    self.mm(Ow, VWIN[:, kt, g, :], PT[:], start=(kt == k0), stop=(kt == i), r=[bVW, bPT], w=[bOw])
                combine(1, Os_, bOs_, False, False)
                combine(2, Ow, bOw, False, True)
                self.dma(D["MIXT"][g * 256:(g + 1) * 256, qs].rearrange("(j p) q -> p j q", p=128), ms[:], r=[bms],
                         w=[self.db("MIXT", i)])


K.phase_nsa = _phase_nsa
```
